# Optimizing a Trainium2 kernel written in Bass

```python
import jax, jax.numpy as jnp
from jax import lax
import numpy as np

D_MODEL = 2048
BATCH = 2
SEQ = 8192
DEPTH = 2

D_MIX = D_MODEL
CONV_W = D_MIX // 4
CONV_K = 3
SB_HEADS = 8
SB_HD = 128
SB_W = SB_HEADS * SB_HD
SB_BLOCK = 128
GLA_HEADS = 4
GLA_W = D_MIX - CONV_W - SB_W
GLA_DV = GLA_W // GLA_HEADS
GLA_DK = GLA_DV // 2
GLA_LR = 16
GLA_TAU = 16.0
GLA_CHUNK = 64
EPS = 1e-6
IN_SPLITS = (CONV_W,) * 4 + (SB_W,) * 4 + (GLA_HEADS * GLA_DK,) * 2 + (GLA_W,) * 2 + (GLA_LR,)
N_IN = sum(IN_SPLITS)

kernel_name = "hybrid_conv_stickbreak_gla_parallel"


def _rmsnorm(x, g):
    xf = x.astype(jnp.float32)
    y = xf * lax.rsqrt(jnp.mean(xf * xf, axis=-1, keepdims=True) + EPS)
    return y * g.astype(jnp.float32)


def _short_conv(h, b_gate, c_gate, w, bias):
    u = c_gate * h
    y = lax.conv_general_dilated(
        u, w[:, None, :], window_strides=(1,), padding=[(CONV_K - 1, 0)],
        dimension_numbers=("NWC", "WIO", "NWC"), feature_group_count=CONV_W)
    return b_gate.astype(jnp.float32) * (y + bias).astype(jnp.float32)


def _stick_breaking(q, k, v):
    b, s, nh, hd = q.shape
    qh = q.transpose(0, 2, 1, 3) * (hd ** -0.5)
    kh = k.transpose(0, 2, 1, 3)
    vh = v.transpose(0, 2, 1, 3)
    kpos = jnp.arange(s)

    def block(i):
        start = i * SB_BLOCK
        qb = lax.dynamic_slice_in_dim(qh, start, SB_BLOCK, axis=2)
        z = jnp.einsum("bhqd,bhkd->bhqk", qb, kh)
        qpos = start + jnp.arange(SB_BLOCK)
        mask = kpos[None, :] < qpos[:, None]
        log_keep = jnp.where(mask, jax.nn.log_sigmoid(-z), 0.0)
        later = lax.cumsum(log_keep, axis=3, reverse=True) - log_keep
        w = jnp.where(mask, jnp.exp(jax.nn.log_sigmoid(z) + later), 0.0)
        return jnp.einsum("bhqk,bhkd->bhqd", w, vh)

    out = lax.map(block, jnp.arange(s // SB_BLOCK))
    return out.transpose(1, 0, 3, 2, 4).reshape(b, s, nh * hd)


def _gla(q, k, v, g):
    b, s, nh, dk = q.shape
    dv = v.shape[-1]
    nc = s // GLA_CHUNK
    q = q * (dk ** -0.5)

    def to_chunks(t):
        return t.reshape(b, nc, GLA_CHUNK, nh, t.shape[-1]).transpose(1, 0, 3, 2, 4)

    causal = jnp.tril(jnp.ones((GLA_CHUNK, GLA_CHUNK), dtype=bool))[:, :, None]

    def step(state, inp):
        qc, kc, vc, gc = inp
        cum = jnp.cumsum(gc, axis=2)
        o_inter = jnp.einsum("bhck,bhkv->bhcv", qc * jnp.exp(cum), state)
        diff = cum[:, :, :, None, :] - cum[:, :, None, :, :]
        decay = jnp.where(causal, jnp.exp(jnp.minimum(diff, 0.0)), 0.0)
        att = jnp.einsum("bhtk,bhsk,bhtsk->bhts", qc, kc, decay)
        o_intra = jnp.einsum("bhts,bhsv->bhtv", att, vc)
        last = cum[:, :, -1:, :]
        k_dec = kc * jnp.exp(last - cum)
        new_state = state * jnp.exp(last[:, :, 0, :, None]) + jnp.einsum("bhsk,bhsv->bhkv", k_dec, vc)
        return new_state, o_inter + o_intra

    init = jnp.zeros((b, nh, dk, dv), jnp.float32)
    _, o = lax.scan(step, init, (to_chunks(q), to_chunks(k), to_chunks(v), to_chunks(g)))
    return o.transpose(1, 0, 3, 2, 4).reshape(b, s, nh, dv)


def _layer(x, norm_g, w_in, conv_w, conv_b, sb_qn_g, sb_kn_g, gla_w_up, gla_b_up, gla_on_g, w_out):
    b, s, _ = x.shape
    h = _rmsnorm(x, norm_g).astype(x.dtype)
    proj = h @ w_in
    split_points = np.cumsum(IN_SPLITS)[:-1].tolist()
    (cv_h, cv_b, cv_c, cv_g,
     sb_q, sb_k, sb_v, sb_g,
     gl_q, gl_k, gl_v, gl_g, gl_lr) = jnp.split(proj, split_points, axis=-1)
    f32 = jnp.float32

    y_conv = _short_conv(cv_h, cv_b, cv_c, conv_w, conv_b) * jax.nn.silu(cv_g.astype(f32))

    q = _rmsnorm(sb_q.reshape(b, s, SB_HEADS, SB_HD), sb_qn_g)
    k = _rmsnorm(sb_k.reshape(b, s, SB_HEADS, SB_HD), sb_kn_g)
    v = sb_v.reshape(b, s, SB_HEADS, SB_HD).astype(f32)
    y_sb = _stick_breaking(q, k, v) * jax.nn.silu(sb_g.astype(f32))

    g = jax.nn.log_sigmoid((gl_lr @ gla_w_up + gla_b_up).astype(f32)) / GLA_TAU
    o = _gla(gl_q.reshape(b, s, GLA_HEADS, GLA_DK).astype(f32),
             gl_k.reshape(b, s, GLA_HEADS, GLA_DK).astype(f32),
             gl_v.reshape(b, s, GLA_HEADS, GLA_DV).astype(f32),
             g.reshape(b, s, GLA_HEADS, GLA_DK))
    y_gla = _rmsnorm(o, gla_on_g).reshape(b, s, GLA_W) * jax.nn.silu(gl_g.astype(f32))

    mix = jnp.concatenate([y_conv, y_sb, y_gla], axis=-1).astype(x.dtype)
    return x + mix @ w_out


def setup_inputs(seed: int = 0) -> dict:
    key = jax.random.key(seed)
    ks = jax.random.split(key, 12)
    f32 = jnp.float32
    n = lambda k, shape: jax.random.normal(k, shape, f32)
    x = n(ks[0], (BATCH, SEQ, D_MODEL))
    norm_g = 1.0 + 0.02 * n(ks[1], (DEPTH, D_MODEL))
    w_in = n(ks[2], (DEPTH, D_MODEL, N_IN)) * D_MODEL ** -0.5
    conv_w = n(ks[3], (DEPTH, CONV_K, CONV_W)) * CONV_K ** -0.5
    conv_b = 0.02 * n(ks[4], (DEPTH, CONV_W))
    sb_qn_g = 1.0 + 0.02 * n(ks[5], (DEPTH, SB_HD))
    sb_kn_g = 1.0 + 0.02 * n(ks[6], (DEPTH, SB_HD))
    gla_w_up = n(ks[7], (DEPTH, GLA_LR, GLA_HEADS * GLA_DK)) * GLA_LR ** -0.5
    gla_b_up = 0.02 * n(ks[8], (DEPTH, GLA_HEADS * GLA_DK))
    gla_on_g = 1.0 + 0.02 * n(ks[9], (DEPTH, GLA_DV))
    w_out = n(ks[10], (DEPTH, D_MIX, D_MODEL)) * (D_MIX ** -0.5) * 0.5
    return {"x": x, "norm_g": norm_g, "w_in": w_in, "conv_w": conv_w, "conv_b": conv_b,
            "sb_qn_g": sb_qn_g, "sb_kn_g": sb_kn_g, "gla_w_up": gla_w_up, "gla_b_up": gla_b_up,
            "gla_on_g": gla_on_g, "w_out": w_out}


def reference(x, norm_g, w_in, conv_w, conv_b, sb_qn_g, sb_kn_g, gla_w_up, gla_b_up, gla_on_g, w_out):
    h = x
    for l in range(DEPTH):
        h = _layer(h, norm_g[l], w_in[l], conv_w[l], conv_b[l], sb_qn_g[l], sb_kn_g[l],
                   gla_w_up[l], gla_b_up[l], gla_on_g[l], w_out[l])
    return h
```

```python
import numpy as np
import ml_dtypes
from contextlib import ExitStack
import concourse.bass as bass
import concourse.mybir as mybir
from concourse.bass_utils import run_bass_kernel_spmd

F32 = mybir.dt.float32
BF16 = mybir.dt.bfloat16
AF = mybir.ActivationFunctionType
ALU = mybir.AluOpType

D = 2048
NCOL = 1936
EPS = 1e-6
NPAR = 72
NCST = 448
DBG_C = 6
DBG_S2 = 9


class Buf:
    __slots__ = ("name", "w", "r", "excl")

    def __init__(self, name="", excl=False):
        self.name = name
        self.w = None
        self.r = {}
        self.excl = excl


class Trk:
    ENG = ("pe", "act", "dve", "pool", "sp")

    def __init__(self, nc, es):
        self.nc = nc
        self.eng = {"pe": nc.tensor, "act": nc.scalar, "dve": nc.vector, "pool": nc.gpsimd, "sp": nc.sync}
        self.sem = {e: es.enter_context(nc.semaphore("sem_" + e)) for e in ("pe", "act", "dve", "pool")}
        self.cnt = {e: 0 for e in self.sem}
        self.dq = {}
        for q, n in (("sp", 8), ("pool", 6)):
            self.dq[q] = {"sems": [es.enter_context(nc.semaphore(f"d_{q}{i}")) for i in range(n)],
                          "cnt": [0] * n, "next": 0}
        self.seen = {e: {} for e in self.ENG}

    def _semh(self, key):
        return self.sem[key] if isinstance(key, str) else self.dq[key[0]]["sems"][key[1]]

    def _wait(self, e, ev):
        key, val, src = ev
        if self.seen[e].get(key, 0) >= val:
            return
        self.eng[e].wait_ge(self._semh(key), val)
        self.seen[e][key] = val

    def _deps(self, e, reads, writes, is_dma):
        evs = []
        for b in reads:
            if b.w is not None:
                evs.append(b.w)
            if b.excl:
                evs.extend(v for k, v in b.r.items() if k != e)
        for b in writes:
            if b.w is not None:
                evs.append(b.w)
            evs.extend(b.r.values())
        for ev in evs:
            self._wait(e, ev)

    def _update(self, ev, e, reads, writes):
        for b in writes:
            b.w = ev
            b.r = {}
        for b in reads:
            if b not in writes:
                b.r[e] = ev

    def op(self, e, fn, reads=(), writes=()):
        self._deps(e, reads, writes, False)
        ins = fn(self.eng[e])
        self.cnt[e] += 1
        ins.then_inc(self.sem[e], 1)
        self._update((e, self.cnt[e], e), e, reads, writes)

    def dma(self, q, out, in_, reads=(), writes=()):
        dq = self.dq[q]
        i = dq["next"]
        dq["next"] = (i + 1) % len(dq["sems"])
        if dq["cnt"][i] > 0:
            self._wait(q, ((q, i), 16 * dq["cnt"][i], "dma"))
        self._deps(q, reads, writes, True)
        ins = self.eng[q].dma_start(out=out, in_=in_)
        dq["cnt"][i] += 1
        ins.then_inc(dq["sems"][i], 16)
        self._update(((q, i), 16 * dq["cnt"][i], "dma"), "dma" + q, reads, writes)

    def barrier(self):
        evs = [(e, self.cnt[e], e) for e in self.sem if self.cnt[e] > 0]
        for q, dq in self.dq.items():
            evs += [((q, i), 16 * c, "dma") for i, c in enumerate(dq["cnt"]) if c > 0]
        for e in self.ENG:
            for ev in evs:
                self._wait(e, ev)


class Ctx:
    pass


def load_consts(T, nc, es, cst_d):
    E = es.enter_context
    c = Ctx()
    c.cf = E(nc.sbuf_tensor("cst_f", [128, NCST], F32))
    c.cb = E(nc.sbuf_tensor("cst_b", [128, NCST], BF16))
    c.zero = E(nc.sbuf_tensor("zeros_f", [128, 512], F32))
    c.B = Buf("cst")
    T.dma("sp", c.cf[:], cst_d[:], writes=[c.B])
    T.op("dve", lambda e: e.tensor_copy(c.cb[:], c.cf[:]), reads=[c.B], writes=[c.B])
    T.op("pool", lambda e: e.memset(c.zero[:], 0.0), writes=[c.B])
    c.ident_b = c.cb[:, 0:128]
    c.maskL_f = c.cf[:, 128:256]
    c.maskL_b = c.cb[:, 128:256]
    c.ones_f = c.cf[:, 256:384]
    c.maskG_f = c.cf[0:64, 384:448]
    return c


def load_params(T, nc, es, par_d, l):
    E = es.enter_context
    p = Ctx()
    p.t = E(nc.sbuf_tensor(f"par{l}", [128, NPAR + 4], F32))
    p.B = Buf("par")
    T.dma("sp", p.t[:, 0:NPAR], par_d[:], writes=[p.B])
    T.op("dve", lambda e: e.tensor_scalar(out=p.t[:, NPAR:NPAR + 1], in0=p.t[:, 4:5], scalar1=float(128 ** -0.5),
                                           scalar2=None, op0=ALU.mult), reads=[p.B], writes=[p.B])
    T.op("dve", lambda e: e.tensor_scalar(out=p.t[:, NPAR + 1:NPAR + 2], in0=p.t[:, 7:8], scalar1=-1.0,
                                           scalar2=None, op0=ALU.mult), reads=[p.B], writes=[p.B])
    p.cw = [p.t[:, 0:1], p.t[:, 1:2], p.t[:, 2:3]]
    p.cb = p.t[:, 3:4]
    p.gq = p.t[:, NPAR:NPAR + 1]
    p.gk = p.t[:, 5:6]
    p.gon = p.t[:, 6:7]
    p.negb = p.t[0:64, NPAR + 1:NPAR + 2]
    p.wup = p.t[0:16, 8:72]
    return p


def phase_A(T, nc, S, cst, par, x_d, win_d, ng_d, sc):
    NT = S // 512
    with ExitStack() as es:
        E = es.enter_context
        wbf = E(nc.sbuf_tensor("wbf", [128, 16, NCOL], BF16))
        ngb = E(nc.sbuf_tensor("ngb", [128, D], F32))
        xt = E(nc.sbuf_tensor("xt", [128, 4, D], F32))
        xn = E(nc.sbuf_tensor("xn", [128, 4, D], BF16))
        junk = E(nc.sbuf_tensor("junk", [128, D], BF16))
        hT = [E(nc.sbuf_tensor(f"hT{i}", [128, 16, 512], BF16)) for i in range(2)]
        ssq = E(nc.sbuf_tensor("ssq", [128, 8], F32))
        tp = [E(nc.psum_tensor(f"tp{i}", [128, 1024], BF16)) for i in range(2)]
        acc = [E(nc.psum_tensor(f"acc{i}", [128, 512], F32)) for i in range(2)]
        ssp = E(nc.psum_tensor("ssp", [128, 512], F32))
        NR = 3
        ch = E(nc.sbuf_tensor("ch", [128, 512], F32))
        u = E(nc.sbuf_tensor("u", [128, 514], F32))
        y1 = E(nc.sbuf_tensor("y1", [128, 512], F32))
        y2 = E(nc.sbuf_tensor("y2", [128, 512], F32))
        sg = E(nc.sbuf_tensor("sg", [128, 512], F32))
        mixc = [E(nc.sbuf_tensor(f"mixc{i}", [128, 512], BF16)) for i in range(2)]
        qf = [E(nc.sbuf_tensor(f"qf{i}", [128, 512], F32)) for i in range(NR)]
        sq = [E(nc.sbuf_tensor(f"sq{i}", [128, 512], F32)) for i in range(NR)]
        rt = [E(nc.sbuf_tensor(f"rt{i}", [128, 512], F32)) for i in range(NR)]
        qn = [E(nc.sbuf_tensor(f"qn{i}", [128, 512], BF16)) for i in range(NR)]
        gout = [E(nc.sbuf_tensor(f"gout{i}", [128, 512], F32)) for i in range(NR)]
        lr = E(nc.sbuf_tensor("lr", [16, 512], F32))
        le = E(nc.sbuf_tensor("le", [64, 512], F32))
        vt = [E(nc.sbuf_tensor(f"vt{i}", [128, 4, 384], BF16)) for i in range(2)]

        B = lambda n: Buf(n)
        b_w = B("wbf"); b_ng = B("ngb")
        b_xt = [B(f"xt{s}") for s in range(4)]
        b_xn = [B(f"xn{s}") for s in range(4)]
        b_junk = B("junk"); b_ssq = B("ssq")
        b_hT = [[B(f"hT{i}_{c}") for c in range(8)] for i in range(2)]
        b_tp = [Buf("tp0", True), Buf("tp1", True)]
        b_acc = [Buf("acc0", True), Buf("acc1", True)]
        b_ssp = Buf("ssp", True)
        b_ch = B("ch"); b_u = B("u"); b_y1 = B("y1"); b_y2 = B("y2"); b_sg = B("sg")
        b_mixc = [B("mixc0"), B("mixc1")]
        b_qf = [B(f"qf{i}") for i in range(NR)]
        b_sq = [B(f"sq{i}") for i in range(NR)]
        b_rt = [B(f"rt{i}") for i in range(NR)]
        b_qn = [B(f"qn{i}") for i in range(NR)]
        b_gout = [B(f"gout{i}") for i in range(NR)]
        b_lr = B("lr"); b_le = B("le")
        b_vt = [B("vt0"), B("vt1")]
        b_dram = B("dramA")

        for c in range(16):
            T.dma("pool", wbf[:, c, :], win_d[c * 128:(c + 1) * 128, :], writes=[b_w])
        T.dma("sp", ngb[:], ng_d[:], writes=[b_ng])
        T.op("dve", lambda e: e.memset(u[:, 0:2], 0.0), writes=[b_u])

        rot = {"acc": 0, "r": 0, "tp": 0}

        def next_acc():
            i = rot["acc"]; rot["acc"] = 1 - i
            return i

        def next_r():
            i = rot["r"]; rot["r"] = (i + 1) % NR
            return i

        for ti in range(NT):
            t0 = ti * 512
            hb = ti % 2
            xv = x_d[t0:t0 + 512, :].rearrange("(s p) d -> p s d", p=128)
            for s in range(4):
                T.dma("sp", xt[:, s, :], xv[:, s, :], writes=[b_xt[s]])
            for s in range(4):
                T.op("act", lambda e, s=s: e.activation(out=junk[:], in_=xt[:, s, :], func=AF.Square,
                                                         accum_out=ssq[:, s:s + 1]),
                     reads=[b_xt[s]], writes=[b_junk, b_ssq])
            T.op("act", lambda e: e.activation(out=ssq[:, 4:8], in_=ssq[:, 0:4], func=AF.Sqrt,
                                               scale=1.0 / D, bias=EPS), reads=[b_ssq], writes=[b_ssq])
            T.op("dve", lambda e: e.reciprocal(out=ssq[:, 4:8], in_=ssq[:, 4:8]), reads=[b_ssq], writes=[b_ssq])
            for s in range(4):
                T.op("dve", lambda e, s=s: e.scalar_tensor_tensor(out=xn[:, s, :], in0=xt[:, s, :],
                                                               scalar=ssq[:, 4 + s:5 + s], in1=ngb[:],
                                                               op0=ALU.mult, op1=ALU.mult),
                     reads=[b_xt[s], b_ssq, b_ng], writes=[b_xn[s]])
            for cp in range(8):
                k = rot["tp"]; rot["tp"] = 1 - k

                def tr(e, cp=cp, k=k):
                    ins = None
                    for j in range(2):
                        c = 2 * cp + j
                        for s in range(4):
                            ins = e.transpose(tp[k][:, j * 512 + s * 128: j * 512 + (s + 1) * 128],
                                              xn[:, s, c * 128:(c + 1) * 128], cst.ident_b)
                    return ins
                T.op("pe", tr, reads=b_xn + [cst.B], writes=[b_tp[k]])
                dst = hT[hb][:, 2 * cp:2 * cp + 2, :].rearrange("p a t -> p (a t)")
                if cp % 2 == 0:
                    T.op("act", lambda e, k=k, dst=dst: e.copy(out=dst, in_=tp[k][:]),
                         reads=[b_tp[k]], writes=[b_hT[hb][cp]])
                else:
                    T.op("dve", lambda e, k=k, dst=dst: e.tensor_copy(dst, tp[k][:]),
                         reads=[b_tp[k]], writes=[b_hT[hb][cp]])

            def proj_fm(col0, ncols):
                a = next_acc()

                def mm(e):
                    ins = None
                    for c in range(16):
                        ins = e.matmul(acc[a][0:ncols, :], wbf[:, c, col0:col0 + ncols], hT[hb][:, c, :],
                                       start=(c == 0), stop=(c == 15))
                    return ins
                T.op("pe", mm, reads=[b_w] + b_hT[hb], writes=[b_acc[a]])
                return a

            cols = slice(t0, t0 + 512)
            a = proj_fm(0, 128)
            T.op("act", lambda e, a=a: e.copy(out=ch[:], in_=acc[a][:]), reads=[b_acc[a]], writes=[b_ch])
            a = proj_fm(128, 128)
            T.op("dve", lambda e, a=a: e.tensor_tensor(out=u[:, 2:514], in0=acc[a][:], in1=ch[:], op=ALU.mult),
                 reads=[b_acc[a], b_ch], writes=[b_u])
            T.op("pool", lambda e: e.tensor_scalar(out=y1[:], in0=u[:, 2:514], scalar1=par.cw[2], scalar2=par.cb,
                                                   op0=ALU.mult, op1=ALU.add), reads=[b_u, par.B], writes=[b_y1])
            T.op("dve", lambda e: e.scalar_tensor_tensor(out=y2[:], in0=u[:, 1:513], scalar=par.cw[1], in1=y1[:],
                                                          op0=ALU.mult, op1=ALU.add),
                 reads=[b_u, b_y1, par.B], writes=[b_y2])
            T.op("dve", lambda e: e.scalar_tensor_tensor(out=y1[:], in0=u[:, 0:512], scalar=par.cw[0], in1=y2[:],
                                                          op0=ALU.mult, op1=ALU.add),
                 reads=[b_u, b_y2, par.B], writes=[b_y1])
            T.op("pool", lambda e: e.tensor_copy(u[:, 0:2], u[:, 512:514]), reads=[b_u], writes=[b_u])
            a = proj_fm(256, 128)
            T.op("dve", lambda e, a=a: e.tensor_tensor(out=y2[:], in0=acc[a][:], in1=y1[:], op=ALU.mult),
                 reads=[b_acc[a], b_y1], writes=[b_y2])
            a = proj_fm(384, 128)
            T.op("act", lambda e, a=a: e.activation(out=sg[:], in_=acc[a][:], func=AF.Silu),
                 reads=[b_acc[a]], writes=[b_sg])
            mk = ti % 2
            T.op("pool", lambda e, mk=mk: e.tensor_tensor(out=mixc[mk][:], in0=y2[:], in1=sg[:], op=ALU.mult),
                 reads=[b_y2, b_sg], writes=[b_mixc[mk]])
            T.dma("pool", sc.mixT[0:128, cols], mixc[mk][:], reads=[b_mixc[mk]], writes=[b_dram])

            for j in range(4):
                a = proj_fm(512 + 128 * j, 128)
                r = next_r()
                T.op("act", lambda e, a=a, r=r: e.copy(out=qf[r][:], in_=acc[a][:]), reads=[b_acc[a]], writes=[b_qf[r]])
                T.op("act", lambda e, a=a, r=r: e.activation(out=sq[r][:], in_=acc[a][:], func=AF.Square),
                     reads=[b_acc[a]], writes=[b_sq[r]])
                T.op("pe", lambda e, r=r: e.matmul(ssp[:], cst.ones_f, sq[r][:], start=True, stop=True),
                     reads=[b_sq[r], cst.B], writes=[b_ssp])
                T.op("act", lambda e, r=r: e.activation(out=rt[r][:], in_=ssp[:], func=AF.Sqrt, scale=1.0 / 128, bias=EPS),
                     reads=[b_ssp], writes=[b_rt[r]])
                T.op("dve", lambda e, r=r: e.reciprocal(out=rt[r][:], in_=rt[r][:]), reads=[b_rt[r]], writes=[b_rt[r]])
                gcol = par.gq if j < 2 else par.gk
                T.op("dve", lambda e, r=r, gcol=gcol: e.scalar_tensor_tensor(out=qn[r][:], in0=qf[r][:], scalar=gcol,
                                                                           in1=rt[r][:], op0=ALU.mult, op1=ALU.mult),
                     reads=[b_qf[r], b_rt[r], par.B], writes=[b_qn[r]])
                dst = (sc.qT if j < 2 else sc.kT)[j % 2, :, cols]
                T.dma("sp", dst, qn[r][:], reads=[b_qn[r]], writes=[b_dram])
            for j in range(2):
                a = proj_fm(1024 + 128 * j, 128)
                r = next_r()
                T.op("act", lambda e, a=a, r=r: e.activation(out=gout[r][:], in_=acc[a][:], func=AF.Silu),
                     reads=[b_acc[a]], writes=[b_gout[r]])
                T.dma("sp", sc.gs[j, :, cols], gout[r][:], reads=[b_gout[r]], writes=[b_dram])
            a = proj_fm(1280, 128)
            r = next_r()
            T.op("dve", lambda e, a=a, r=r: e.tensor_copy(gout[r][:], acc[a][:]), reads=[b_acc[a]], writes=[b_gout[r]])
            T.dma("sp", sc.gqk[:, cols], gout[r][:], reads=[b_gout[r]], writes=[b_dram])
            a = proj_fm(1408, 128)
            r = next_r()
            T.op("act", lambda e, a=a, r=r: e.activation(out=gout[r][:], in_=acc[a][:], func=AF.Silu),
                 reads=[b_acc[a]], writes=[b_gout[r]])
            T.dma("sp", sc.ggate[:, cols], gout[r][:], reads=[b_gout[r]], writes=[b_dram])
            a = proj_fm(1536, 16)
            T.op("act", lambda e, a=a: e.copy(out=lr[:], in_=acc[a][0:16, :]), reads=[b_acc[a]], writes=[b_lr])
            T.op("pe", lambda e: e.matmul(ssp[0:64, :], par.wup, lr[:], start=True, stop=True),
                 reads=[b_lr, par.B], writes=[b_ssp])
            T.op("act", lambda e: e.activation(out=le[:], in_=ssp[0:64, :], func=AF.Exp, scale=-1.0, bias=par.negb),
                 reads=[b_ssp, par.B], writes=[b_le])
            T.op("act", lambda e: e.activation(out=le[:], in_=le[:], func=AF.Ln, bias=1.0), reads=[b_le], writes=[b_le])
            r = next_r()
            T.op("dve", lambda e, r=r: e.tensor_scalar(out=gout[r][0:64, :], in0=le[:], scalar1=-1.0 / 16.0, scalar2=None,
                                                       op0=ALU.mult), reads=[b_le], writes=[b_gout[r]])
            T.dma("sp", sc.glog[:, cols], gout[r][0:64, :], reads=[b_gout[r]], writes=[b_dram])
            vk = ti % 2
            for s in range(4):
                a = next_acc()

                def mmv(e, a=a, s=s):
                    ins = None
                    for c in range(16):
                        ins = e.matmul(acc[a][:, 0:384], hT[hb][:, c, s * 128:(s + 1) * 128], wbf[:, c, 1552:1936],
                                       start=(c == 0), stop=(c == 15))
                    return ins
                T.op("pe", mmv, reads=[b_w] + b_hT[hb], writes=[b_acc[a]])
                if s % 2 == 0:
                    T.op("dve", lambda e, a=a, s=s: e.tensor_copy(vt[vk][:, s, :], acc[a][:, 0:384]),
                         reads=[b_acc[a]], writes=[b_vt[vk]])
                else:
                    T.op("act", lambda e, a=a, s=s: e.copy(out=vt[vk][:, s, :], in_=acc[a][:, 0:384]),
                         reads=[b_acc[a]], writes=[b_vt[vk]])
            T.dma("sp", sc.vtok[t0:t0 + 512, :].rearrange("(s p) c -> p s c", p=128), vt[vk][:],
                  reads=[b_vt[vk]], writes=[b_dram])
        T.barrier()


def phase_C(T, nc, S, cst, sc):
    NQ = S // 128
    with ExitStack() as es:
        E = es.enter_context
        qT = E(nc.sbuf_tensor("qT", [128, S], BF16))
        kT = E(nc.sbuf_tensor("kT", [128, S], BF16))
        vv = E(nc.sbuf_tensor("vv", [128, NQ, 128], BF16))
        gate = E(nc.sbuf_tensor("gate", [128, S], F32))
        mixb = E(nc.sbuf_tensor("mixb", [128, S], BF16))
        NZ, NS = 3, 3
        zps = [E(nc.psum_tensor(f"zps{i}", [128, 512], F32)) for i in range(NZ)]
        wtp = [E(nc.psum_tensor(f"wtp{i}", [128, 1024], BF16)) for i in range(2)]
        otp = [E(nc.psum_tensor(f"otp{i}", [128, 512], F32)) for i in range(2)]
        eb = [E(nc.sbuf_tensor(f"eb{i}", [128, 512], F32)) for i in range(NS)]
        sm = [E(nc.sbuf_tensor(f"sm{i}", [128, 516], F32)) for i in range(NS)]
        G = [E(nc.sbuf_tensor(f"G{i}", [128, 512], F32)) for i in range(NS)]
        wsb = [E(nc.sbuf_tensor(f"wsb{i}", [128, 512], BF16)) for i in range(NS)]
        wts = [E(nc.sbuf_tensor(f"wts{i}", [128, 512], BF16)) for i in range(NS)]
        zm = E(nc.sbuf_tensor("zm", [128, 128], F32))
        ngc = E(nc.sbuf_tensor("ngc", [128, 16], F32))
        tq = E(nc.sbuf_tensor("tq", [128, 4], F32))

        b_in = Buf("cin")
        b_z = [Buf(f"z{i}", True) for i in range(NZ)]
        b_wtp = [Buf("wtp0", True), Buf("wtp1", True)]
        b_otp = [Buf("otp0", True), Buf("otp1", True)]
        b_eb = [Buf(f"eb{i}") for i in range(NS)]
        b_sm = [Buf(f"sm{i}") for i in range(NS)]
        b_G = [Buf(f"G{i}") for i in range(NS)]
        b_w = [Buf(f"w{i}") for i in range(NS)]
        b_wts = [Buf(f"wts{i}") for i in range(NS)]
        b_zm = Buf("zm")
        b_ngc = [Buf(f"ngc{i}") for i in range(8)]
        b_tq = Buf("tq")
        b_mix = Buf("mixb")
        b_dram = Buf("dramC")

        for i in range(NS):
            T.op("dve", lambda e, i=i: e.memset(sm[i][:, 0:1], 0.0), writes=[b_sm[i]])
        T.op("dve", lambda e: e.memset(ngc[:, 8:9], 0.0), writes=[b_ngc[0]])

        for h in range(2):
            T.dma("sp", qT[:], sc.qT[h], writes=[b_in])
            T.dma("sp", kT[:], sc.kT[h], writes=[b_in])
            T.dma("sp", vv[:], sc.vtok[:, h * 128:(h + 1) * 128].rearrange("(n p) d -> p n d", p=128), writes=[b_in])
            T.dma("sp", gate[:], sc.gs[h], writes=[b_in])

            tiles = []
            for qb in range(NQ):
                lst = [(qb, qb * 128, 128, True)]
                khi = qb * 128
                while khi > 0:
                    W = min(512, khi)
                    lst.append((qb, khi - W, W, False))
                    khi -= W
                for i, t in enumerate(lst):
                    tiles.append(t + (i == 0, i == len(lst) - 1))
            n = len(tiles)
            st = {}

            def s0(i):
                qb, k0, W, dg, first, last = tiles[i]
                zi = i % NZ
                T.op("pe", lambda e: e.matmul(zps[zi][:, 0:W], qT[:, qb * 128:(qb + 1) * 128], kT[:, k0:k0 + W],
                                              start=True, stop=True), reads=[b_in], writes=[b_z[zi]])

            def s1(i):
                qb, k0, W, dg, first, last = tiles[i]
                zi = i % NZ; r = i % NS
                T.op("act", lambda e: e.activation(out=eb[r][:, 0:W], in_=zps[zi][:, 0:W], func=AF.Exp, scale=-1.0),
                     reads=[b_z[zi]], writes=[b_eb[r]])
                T.op("act", lambda e: e.activation(out=sm[r][:, 1:W + 1], in_=eb[r][:, 0:W], func=AF.Ln, bias=1.0),
                     reads=[b_eb[r]], writes=[b_sm[r]])

            def s2(i):
                qb, k0, W, dg, first, last = tiles[i]
                zi = i % NZ; r = i % NS
                if dg:
                    T.op("dve", lambda e: e.tensor_tensor(out=sm[r][:, 1:129], in0=sm[r][:, 1:129], in1=cst.maskL_f,
                                                          op=ALU.mult), reads=[b_sm[r], cst.B], writes=[b_sm[r]])
                    T.op("dve", lambda e: e.tensor_tensor(out=zm[:], in0=zps[zi][:, 0:128], in1=cst.maskL_f, op=ALU.mult),
                         reads=[b_z[zi], b_sm[r], cst.B], writes=[b_zm])
                    zsrc, zb = zm[:], b_zm
                else:
                    zsrc, zb = zps[zi][:, 0:W], b_z[zi]
                if DBG_S2 < 1:
                    return
                T.op("dve", lambda e: e.tensor_tensor_scan(out=G[r][:, 0:W], data0=sm[r][:, 0:W], data1=zsrc, initial=0.0,
                                                           op0=ALU.add, op1=ALU.add),
                     reads=[b_sm[r], zb], writes=[b_G[r]])
                if DBG_S2 < 2:
                    return
                ci = i % 8
                prev = ngc[:, 8:9] if first else ngc[:, (i - 1) % 8:(i - 1) % 8 + 1]
                pb = b_ngc[0] if first else b_ngc[(i - 1) % 8]
                T.op("dve", lambda e: e.tensor_tensor(out=tq[:, 0:1], in0=G[r][:, W - 1:W], in1=sm[r][:, W:W + 1], op=ALU.add),
                     reads=[b_G[r], b_sm[r]], writes=[b_tq])
                if DBG_S2 < 3:
                    return
                T.op("dve", lambda e: e.tensor_tensor(out=ngc[:, ci:ci + 1], in0=prev, in1=tq[:, 0:1], op=ALU.subtract),
                     reads=[b_tq, pb], writes=[b_ngc[ci]])

            def s3(i):
                qb, k0, W, dg, first, last = tiles[i]
                r = i % NS; ci = i % 8
                T.op("act", lambda e: e.activation(out=wsb[r][:, 0:W], in_=G[r][:, 0:W], func=AF.Exp, bias=ngc[:, ci:ci + 1]),
                     reads=[b_G[r], b_ngc[ci]], writes=[b_w[r]])
                if dg:
                    T.op("pool", lambda e: e.tensor_tensor(out=wsb[r][:, 0:128], in0=wsb[r][:, 0:128], in1=cst.maskL_b,
                                                           op=ALU.mult), reads=[b_w[r], cst.B], writes=[b_w[r]])

            def s4(i):
                qb, k0, W, dg, first, last = tiles[i]
                r = i % NS; k = i % 2

                def tr(e):
                    ins = None
                    for j in range(W // 128):
                        ins = e.transpose(wtp[k][:, j * 128:(j + 1) * 128], wsb[r][:, j * 128:(j + 1) * 128], cst.ident_b)
                    return ins
                T.op("pe", tr, reads=[b_w[r], cst.B], writes=[b_wtp[k]])
                T.op("dve", lambda e: e.tensor_copy(wts[r][:, 0:W], wtp[k][:, 0:W]), reads=[b_wtp[k]], writes=[b_wts[r]])

            def s5(i):
                qb, k0, W, dg, first, last = tiles[i]
                r = i % NS; ob = qb % 2

                def mm(e):
                    ins = None
                    nb = W // 128
                    for j in range(nb):
                        ins = e.matmul(otp[ob][:, 0:128], vv[:, k0 // 128 + j, :], wts[r][:, j * 128:(j + 1) * 128],
                                       start=(first and j == 0), stop=(last and j == nb - 1))
                    return ins
                T.op("pe", mm, reads=[b_wts[r], b_in], writes=[b_otp[ob]])
                if last:
                    T.op("dve", lambda e: e.tensor_tensor(out=mixb[:, qb * 128:(qb + 1) * 128], in0=otp[ob][:, 0:128],
                                                          in1=gate[:, qb * 128:(qb + 1) * 128], op=ALU.mult),
                         reads=[b_otp[ob], b_in], writes=[b_mix])

            stages = [(s0, 0), (s1, 1), (s2, 1), (s3, 2), (s4, 3), (s5, 4)][:DBG_C]
            for step in range(n + 4):
                for fn, lag in stages:
                    i = step - lag
                    if 0 <= i < n:
                        fn(i)
            T.dma("pool", sc.mixT[128 + 128 * h:256 + 128 * h, :], mixb[:], reads=[b_mix], writes=[b_dram])
        T.barrier()


def phase_D(T, nc, S, cst, par, sc):
    NT = S // 512
    with ExitStack() as es:
        E = es.enter_context
        NB = 2
        g_sb = [E(nc.sbuf_tensor(f"g_sb{i}", [64, 512], F32)) for i in range(NB)]
        gq = [E(nc.sbuf_tensor(f"gq{i}", [64, 512], F32)) for i in range(NB)]
        gk = [E(nc.sbuf_tensor(f"gk{i}", [64, 512], F32)) for i in range(NB)]
        gv = [E(nc.sbuf_tensor(f"gv{i}", [64, 8, 128], BF16)) for i in range(NB)]
        gg = [E(nc.sbuf_tensor(f"gg{i}", [128, 512], F32)) for i in range(NB)]
        Gc = E(nc.sbuf_tensor("Gc", [64, 516], F32))
        cum = E(nc.sbuf_tensor("cum", [64, 512], F32))
        Eq = E(nc.sbuf_tensor("Eq", [64, 512], F32))
        Ek = E(nc.sbuf_tensor("Ek", [64, 512], F32))
        El = E(nc.sbuf_tensor("El", [64, 8], F32))
        qe = E(nc.sbuf_tensor("qe", [64, 512], BF16))
        ke = E(nc.sbuf_tensor("ke", [64, 512], BF16))
        kd = E(nc.sbuf_tensor("kd", [64, 512], BF16))
        kdt = E(nc.sbuf_tensor("kdt", [64, 512], BF16))
        att = E(nc.sbuf_tensor("att", [64, 512], BF16))
        stf = E(nc.sbuf_tensor("stf", [64, 128], F32))
        stb = E(nc.sbuf_tensor("stb", [64, 8, 128], BF16))
        osq = E(nc.sbuf_tensor("osq", [128, 512], F32))
        ort = E(nc.sbuf_tensor("ort", [128, 512], F32))
        oy = E(nc.sbuf_tensor("oy", [128, 512], F32))
        oy2 = E(nc.sbuf_tensor("oy2", [128, 512], F32))
        mixg = [E(nc.sbuf_tensor(f"mixg{i}", [128, 512], BF16)) for i in range(2)]
        attp = E(nc.psum_tensor("attp", [64, 512], F32))
        kdtp = E(nc.psum_tensor("kdtp", [64, 1024], BF16))
        up = E(nc.psum_tensor("up", [64, 8, 128], F32))
        op_ = E(nc.psum_tensor("op", [128, 512], F32))
        ssp = E(nc.psum_tensor("sspD", [128, 512], F32))

        b_ld = [Buf(f"ld{i}") for i in range(NB)]
        b_Gc = Buf("Gc"); b_cum = Buf("cum"); b_Eq = Buf("Eq"); b_Ek = Buf("Ek"); b_El = Buf("El")
        b_qe = Buf("qe"); b_ke = Buf("ke"); b_kd = Buf("kd"); b_kdt = Buf("kdt"); b_att = Buf("att")
        b_stf = Buf("stf"); b_stb = Buf("stb")
        b_osq = Buf("osq"); b_ort = Buf("ort"); b_oy = Buf("oy"); b_oy2 = Buf("oy2")
        b_mixg = [Buf("mixg0"), Buf("mixg1")]
        b_attp = Buf("attp", True); b_kdtp = Buf("kdtp", True); b_up = Buf("up", True); b_op = Buf("op", True); b_ssp = Buf("sspD", True)
        b_dram = Buf("dramD")

        T.op("dve", lambda e: e.memset(Gc[:, 0:1], 0.0), writes=[b_Gc])
        T.op("dve", lambda e: e.memset(stf[:], 0.0), writes=[b_stf])
        T.op("pool", lambda e: e.memset(stb[:, 0, :], 0.0), writes=[b_stb])

        def load(ti):
            k = ti % NB
            cols = slice(ti * 512, ti * 512 + 512)
            T.dma("sp", g_sb[k][:], sc.glog[:, cols], writes=[b_ld[k]])
            T.dma("sp", gq[k][:], sc.gqk[0:64, cols], writes=[b_ld[k]])
            T.dma("sp", gk[k][:], sc.gqk[64:128, cols], writes=[b_ld[k]])
            T.dma("sp", gv[k][:], sc.vtok[ti * 512:ti * 512 + 512, 256:384].rearrange("(c p) d -> p c d", p=64),
                  writes=[b_ld[k]])
            T.dma("sp", gg[k][:], sc.ggate[:, cols], writes=[b_ld[k]])

        load(0)
        for ti in range(NT):
            k = ti % NB
            if ti + 1 < NT:
                load(ti + 1)
            L = [b_ld[k]]
            T.op("dve", lambda e: e.tensor_tensor_scan(out=Gc[:, 1:513], data0=g_sb[k][:], data1=cst.zero[0:64, :],
                                                       initial=0.0, op0=ALU.add, op1=ALU.add),
                 reads=L + [cst.B], writes=[b_Gc])
            c3 = lambda ap: ap.rearrange("p (c i) -> p c i", i=64)
            T.op("dve", lambda e: e.tensor_tensor(out=c3(cum[:]), in0=c3(Gc[:, 1:513]),
                                                  in1=c3(Gc[:, 0:512])[:, :, 0:1].to_broadcast([64, 8, 64]),
                                                  op=ALU.subtract), reads=[b_Gc], writes=[b_cum])
            T.op("act", lambda e: e.activation(out=Eq[:], in_=cum[:], func=AF.Exp), reads=[b_cum], writes=[b_Eq])
            T.op("act", lambda e: e.activation(out=Ek[:], in_=cum[:], func=AF.Exp, scale=-1.0), reads=[b_cum], writes=[b_Ek])
            T.op("dve", lambda e: e.tensor_copy(El[:], c3(Eq[:])[:, :, 63]), reads=[b_Eq], writes=[b_El])
            T.op("dve", lambda e: e.scalar_tensor_tensor(out=qe[:], in0=gq[k][:], scalar=0.125, in1=Eq[:],
                                                         op0=ALU.mult, op1=ALU.mult), reads=L + [b_Eq], writes=[b_qe])
            T.op("pool", lambda e: e.tensor_tensor(out=ke[:], in0=gk[k][:], in1=Ek[:], op=ALU.mult),
                 reads=L + [b_Ek], writes=[b_ke])
            T.op("pool", lambda e: e.tensor_tensor(out=Ek[:], in0=gk[k][:], in1=Ek[:], op=ALU.mult),
                 reads=L + [b_Ek, b_ke], writes=[b_Ek])
            T.op("dve", lambda e: e.tensor_tensor(out=c3(kd[:]), in0=c3(Ek[:]),
                                                  in1=El[:].unsqueeze(2).to_broadcast([64, 8, 64]), op=ALU.mult),
                 reads=[b_Ek, b_El], writes=[b_kd])

            def mm_att(e):
                ins = None
                for c in range(8):
                    ins = e.matmul(attp[:, c * 64:(c + 1) * 64], ke[:, c * 64:(c + 1) * 64], qe[:, c * 64:(c + 1) * 64],
                                   start=True, stop=True)
                return ins
            T.op("pe", mm_att, reads=[b_ke, b_qe], writes=[b_attp])
            T.op("dve", lambda e: e.tensor_tensor(out=c3(att[:]), in0=c3(attp[:]),
                                                  in1=cst.maskG_f.unsqueeze(1).to_broadcast([64, 8, 64]), op=ALU.mult),
                 reads=[b_attp, cst.B], writes=[b_att])

            def tr_kd(e):
                ins = None
                for c in range(8):
                    ins = e.transpose(kdtp[:, c * 64:(c + 1) * 64], kd[:, c * 64:(c + 1) * 64], cst.ident_b[0:64, 0:64])
                return ins
            T.op("pe", tr_kd, reads=[b_kd, cst.B], writes=[b_kdtp])
            T.op("act", lambda e: e.copy(out=kdt[:], in_=kdtp[:, 0:512]), reads=[b_kdtp], writes=[b_kdt])

            def mm_u(e):
                ins = None
                for c in range(8):
                    ins = e.matmul(up[:, c, :], kdt[:, c * 64:(c + 1) * 64], gv[k][:, c, :], start=True, stop=True)
                return ins
            T.op("pe", mm_u, reads=[b_kdt] + L, writes=[b_up])
            for c in range(8):
                T.op("dve", lambda e, c=c: e.scalar_tensor_tensor(out=stf[:], in0=stf[:], scalar=El[:, c:c + 1],
                                                                 in1=up[:, c, :], op0=ALU.mult, op1=ALU.add),
                     reads=[b_stf, b_El, b_up], writes=[b_stf])
                if c < 7:
                    T.op("pool", lambda e, c=c: e.tensor_copy(stb[:, c + 1, :], stf[:]), reads=[b_stf], writes=[b_stb])

            def mm_o(e):
                ins = None
                for c in range(8):
                    e.matmul(op_[:, c * 64:(c + 1) * 64], stb[:, c, :], qe[:, c * 64:(c + 1) * 64], start=True, stop=False)
                    ins = e.matmul(op_[:, c * 64:(c + 1) * 64], gv[k][:, c, :], att[:, c * 64:(c + 1) * 64],
                                   start=False, stop=True)
                return ins
            T.op("pe", mm_o, reads=[b_stb, b_qe, b_att] + L, writes=[b_op])
            T.op("pool", lambda e: e.tensor_copy(stb[:, 0, :], stf[:]), reads=[b_stf, b_op], writes=[b_stb])
            T.op("act", lambda e: e.activation(out=osq[:], in_=op_[:], func=AF.Square), reads=[b_op], writes=[b_osq])
            T.op("pe", lambda e: e.matmul(ssp[:], cst.ones_f, osq[:], start=True, stop=True), reads=[b_osq, cst.B],
                 writes=[b_ssp])
            T.op("act", lambda e: e.activation(out=ort[:], in_=ssp[:], func=AF.Sqrt, scale=1.0 / 128, bias=EPS),
                 reads=[b_ssp], writes=[b_ort])
            T.op("dve", lambda e: e.reciprocal(out=ort[:], in_=ort[:]), reads=[b_ort], writes=[b_ort])
            T.op("dve", lambda e: e.scalar_tensor_tensor(out=oy[:], in0=op_[:], scalar=par.gon, in1=ort[:],
                                                         op0=ALU.mult, op1=ALU.mult),
                 reads=[b_op, b_ort, par.B], writes=[b_oy])
            mk = ti % 2
            T.op("pool", lambda e: e.tensor_tensor(out=mixg[mk][:], in0=oy[:], in1=gg[k][:], op=ALU.mult),
                 reads=[b_oy] + L, writes=[b_mixg[mk]])
            T.dma("pool", sc.mixT[384:512, ti * 512:ti * 512 + 512], mixg[mk][:], reads=[b_mixg[mk]], writes=[b_dram])
        T.barrier()


def phase_E(T, nc, NTOK, mixT_full, x_d, wout_d, out_d):
    NT = NTOK // 512
    with ExitStack() as es:
        E = es.enter_context
        wo = E(nc.sbuf_tensor("wo", [128, 16, D], BF16))
        mx = [E(nc.sbuf_tensor(f"mx{i}", [128, 16, 512], BF16)) for i in range(2)]
        xr = [E(nc.sbuf_tensor(f"xr{i}", [128, D], F32)) for i in range(2)]
        ot = [E(nc.sbuf_tensor(f"ot{i}", [128, D], F32)) for i in range(2)]
        acc = [E(nc.psum_tensor(f"accE{i}", [128, 512], F32)) for i in range(4)]
        b_wo = Buf("wo"); b_mx = [Buf("mx0"), Buf("mx1")]; b_xr = [Buf("xr0"), Buf("xr1")]
        b_ot = [Buf("ot0"), Buf("ot1")]; b_acc = [Buf(f"accE{i}", True) for i in range(4)]
        b_out = Buf("outE")
        for c in range(16):
            T.dma("pool", wo[:, c, :], wout_d[c * 128:(c + 1) * 128, :], writes=[b_wo])

        def load_mx(ti):
            k = ti % 2
            T.dma("sp", mx[k][:], mixT_full[:, ti * 512:(ti + 1) * 512].rearrange("(c p) t -> p c t", p=128),
                  writes=[b_mx[k]])
        load_mx(0)
        ai = 0
        for ti in range(NT):
            k = ti % 2
            if ti + 1 < NT:
                load_mx(ti + 1)
            for s in range(4):
                tb = ti * 4 + s
                xk = tb % 2
                rows = slice(tb * 128, (tb + 1) * 128)
                T.dma("sp", xr[xk][:], x_d[rows, :], writes=[b_xr[xk]])
                for n4 in range(4):
                    a = ai; ai = (ai + 1) % 4

                    def mm(e, a=a, n4=n4, s=s):
                        ins = None
                        for c in range(16):
                            ins = e.matmul(acc[a][:], mx[k][:, c, s * 128:(s + 1) * 128], wo[:, c, n4 * 512:(n4 + 1) * 512],
                                           start=(c == 0), stop=(c == 15))
                        return ins
                    T.op("pe", mm, reads=[b_wo, b_mx[k]], writes=[b_acc[a]])
                    T.op("dve", lambda e, a=a, n4=n4, xk=xk: e.tensor_tensor(out=ot[xk][:, n4 * 512:(n4 + 1) * 512],
                                                                            in0=acc[a][:], in1=xr[xk][:, n4 * 512:(n4 + 1) * 512],
                                                                            op=ALU.add),
                         reads=[b_acc[a], b_xr[xk]], writes=[b_ot[xk]])
                T.dma("pool", out_d[rows, :], ot[xk][:], reads=[b_ot[xk]], writes=[b_out])
        T.barrier()


def make_scratch(nc, S, l, mix_kind="Internal"):
    sc = Ctx()
    dt = lambda n, sh, ty, kind="Internal": nc.dram_tensor(n, sh, ty, kind=kind).ap()
    sc.qT = dt(f"s_qT{l}", [2, 128, S], BF16)
    sc.kT = dt(f"s_kT{l}", [2, 128, S], BF16)
    sc.vtok = dt(f"s_vtok{l}", [S, 384], BF16)
    sc.gs = dt(f"s_gs{l}", [2, 128, S], F32)
    sc.gqk = dt(f"s_gqk{l}", [128, S], F32)
    sc.ggate = dt(f"s_ggate{l}", [128, S], F32)
    sc.glog = dt(f"s_glog{l}", [64, S], F32)
    sc.mixT = dt(f"mixT{l}", [512, S], BF16, mix_kind)
    return sc


def build_mixer(S, phases="ACD"):
    nc = bass.Bass("TRN2", target_bir_lowering=False)
    x_d = nc.dram_tensor("x", [S, D], F32, kind="ExternalInput").ap()
    win_d = nc.dram_tensor("win", [D, NCOL], F32, kind="ExternalInput").ap()
    ng_d = nc.dram_tensor("ng", [128, D], F32, kind="ExternalInput").ap()
    par_d = nc.dram_tensor("par", [128, NPAR], F32, kind="ExternalInput").ap()
    cst_d = nc.dram_tensor("cst", [128, NCST], F32, kind="ExternalInput").ap()
    sc = make_scratch(nc, S, 0, "ExternalOutput")
    with ExitStack() as es:
        T = Trk(nc, es)
        cst = load_consts(T, nc, es, cst_d)
        par = load_params(T, nc, es, par_d, 0)
        if "A" in phases:
            phase_A(T, nc, S, cst, par, x_d, win_d, ng_d, sc)
        if "C" in phases:
            phase_C(T, nc, S, cst, sc)
        if "D" in phases:
            phase_D(T, nc, S, cst, par, sc)
        T.barrier()
    return nc


def build_outproj(NTOK):
    nc = bass.Bass("TRN2", target_bir_lowering=False)
    mix_d = nc.dram_tensor("mixf", [D, NTOK], BF16, kind="ExternalInput").ap()
    x_d = nc.dram_tensor("x", [NTOK, D], F32, kind="ExternalInput").ap()
    wout_d = nc.dram_tensor("wout", [D, D], F32, kind="ExternalInput").ap()
    out_d = nc.dram_tensor("xout", [NTOK, D], F32, kind="ExternalOutput").ap()
    with ExitStack() as es:
        T = Trk(nc, es)
        phase_E(T, nc, NTOK, mix_d, x_d, wout_d, out_d)
        T.barrier()
    return nc


def consts_np():
    c = np.zeros((128, NCST), np.float32)
    c[:, 0:128] = np.eye(128, dtype=np.float32)
    p = np.arange(128)[:, None]; cc = np.arange(128)[None, :]
    c[:, 128:256] = (cc < p).astype(np.float32)
    c[:, 256:384] = 1.0
    s = np.arange(64)[:, None]; t = np.arange(64)[None, :]
    c[0:64, 384:448] = (s <= t).astype(np.float32)
    return c


_SPL = np.cumsum([0, 512, 512, 512, 512, 1024, 1024, 1024, 1024, 256, 256, 512, 512, 16])


def win_cols(r):
    o = {n: _SPL[i] for i, n in enumerate(["cv_h", "cv_b", "cv_c", "cv_g", "sb_q", "sb_k", "sb_v", "sb_g",
                                           "gl_q", "gl_k", "gl_v", "gl_g", "gl_lr"])}
    rg = lambda base, start, n: np.arange(base + start, base + start + n)
    idx = [rg(o["cv_h"], 128 * r, 128), rg(o["cv_c"], 128 * r, 128), rg(o["cv_b"], 128 * r, 128), rg(o["cv_g"], 128 * r, 128),
           rg(o["sb_q"], 256 * r, 256), rg(o["sb_k"], 256 * r, 256), rg(o["sb_g"], 256 * r, 256),
           rg(o["gl_q"], 64 * r, 64), rg(o["gl_k"], 64 * r, 64), rg(o["gl_g"], 128 * r, 128), rg(o["gl_lr"], 0, 16),
           rg(o["sb_v"], 256 * r, 256), rg(o["gl_v"], 128 * r, 128)]
    return np.concatenate(idx)


def wout_rows():
    idx = []
    for r in range(4):
        idx += [np.arange(128 * r, 128 * r + 128), np.arange(512 + 256 * r, 512 + 256 * r + 256),
                np.arange(1536 + 128 * r, 1536 + 128 * r + 128)]
    return np.concatenate(idx)


def par_np(inp, l, r):
    p = np.zeros((128, NPAR), np.float32)
    p[:, 0:3] = inp["conv_w"][l][:, 128 * r:128 * r + 128].T
    p[:, 3] = inp["conv_b"][l][128 * r:128 * r + 128]
    p[:, 4] = inp["sb_qn_g"][l]
    p[:, 5] = inp["sb_kn_g"][l]
    p[:, 6] = inp["gla_on_g"][l]
    p[0:64, 7] = inp["gla_b_up"][l][64 * r:64 * r + 64]
    p[0:16, 8:72] = inp["gla_w_up"][l][:, 64 * r:64 * r + 64]
    return p


_CACHE = {}


def _get(key, fn):
    if key not in _CACHE:
        _CACHE[key] = fn()
    return _CACHE[key]


def kernel(**inp):
    x = np.ascontiguousarray(inp["x"], dtype=np.float32)
    Bn, S, _ = x.shape
    cst = consts_np()
    xcur = x
    for l in range(2):
        nc_m = _get(("mix", S), lambda: build_mixer(S))
        in_maps = []
        for core in range(8):
            b, r = core // 4, core % 4
            in_maps.append({
                "x": np.ascontiguousarray(xcur[b]),
                "win": np.ascontiguousarray(inp["w_in"][l][:, win_cols(r)]),
                "ng": np.ascontiguousarray(np.broadcast_to(inp["norm_g"][l][None, :], (128, D))),
                "par": par_np(inp, l, r),
                "cst": cst,
            })
        res = run_bass_kernel_spmd(nc_m, in_maps, core_ids=list(range(8)))
        mixT = [np.concatenate([np.asarray(res.results[b * 4 + r]["mixT0"]) for r in range(4)], axis=0) for b in range(Bn)]
        nc_e = _get(("out", S // 4), lambda: build_outproj(S // 4))
        wout = np.ascontiguousarray(inp["w_out"][l][wout_rows(), :])
        in_maps = []
        Q = S // 4
        for core in range(8):
            b, r = core // 4, core % 4
            in_maps.append({
                "mixf": np.ascontiguousarray(mixT[b][:, r * Q:(r + 1) * Q]),
                "x": np.ascontiguousarray(xcur[b, r * Q:(r + 1) * Q]),
                "wout": wout,
            })
        res = run_bass_kernel_spmd(nc_e, in_maps, core_ids=list(range(8)))
        xcur = np.stack([np.concatenate([np.asarray(res.results[b * 4 + r]["xout"]) for r in range(4)], axis=0)
                         for b in range(Bn)], axis=0)
    return xcur.astype(np.float32)
```
